# Optimizing a Trainium2 kernel written in Bass

```python
import jax, jax.numpy as jnp
from jax import lax
import numpy as np

D_MODEL = 1024
BATCH = 16
SEQ = 256
DEPTH = 2
DEC_BATCH = 4
DEC_SEQ = 1024
PAST_LEN = 256

GRID_W = 64
N_EVEN = (DEPTH + 1) // 2
N_ODD = DEPTH // 2
H_A = 4
DK_A = 128
DV_A = 128
H_B = 4
DK_B = 128
DV_B = 128
SHORT_CONV = 5
C_CONV = H_B * (2 * DK_B + DV_B)
CHUNK = 32
H_C = 16
HD_C = 64
KH_MAX = 8
KW = 16
QBW = 16
KBW = QBW + KW
Q_BLOCK = 128
D_FF = 4 * D_MODEL
EPS = 1e-6
NEG_INF = -1e30

_AB_SIZES = (H_A * DK_A, H_A * DK_A, H_A * DK_A, H_A * DV_A, H_A * DV_A,
             C_CONV, H_B * DV_B, H_B, H_B, H_B, H_B)
D_IN_AB = 3 * H_A * DK_A + 2 * H_A * DV_A + C_CONV + H_B * DV_B + 4 * H_B
D_MIX_AB = H_A * DV_A + H_B * DV_B

kernel_name = 'hybrid_hgrn2_gdn_natten_prefix_dit'


def _split(x, sizes):
    offs, acc = [], 0
    for s in sizes[:-1]:
        acc += s
        offs.append(acc)
    return jnp.split(x, offs, axis=-1)


def _rmsnorm(x, g):
    x32 = x.astype(jnp.float32)
    y = x32 * lax.rsqrt(jnp.mean(x32 * x32, axis=-1, keepdims=True) + EPS)
    return (y * g.astype(jnp.float32)).astype(x.dtype)


def _l2norm(x):
    return x * lax.rsqrt(jnp.sum(x * x, axis=-1, keepdims=True) + EPS)


def _heads(x, h):
    b, t, _ = x.shape
    return x.reshape(b, t, h, -1).transpose(0, 2, 1, 3)


def _merge(x):
    b, h, t, d = x.shape
    return x.transpose(0, 2, 1, 3).reshape(b, t, h * d)


def _modulation(cond, w, b):
    m = jax.nn.silu(cond) @ w + b
    return jnp.split(m[:, None, :], 6, axis=-1)


def _short_conv(x, w):
    return lax.conv_general_dilated(
        x, w.astype(x.dtype), window_strides=(1,),
        padding=[(SHORT_CONV // 2, SHORT_CONV // 2)],
        dimension_numbers=('NWC', 'WIO', 'NWC'), feature_group_count=x.shape[-1])


def _gla_chunked(q, k, v, log_f, s0):
    b, h, t, _ = q.shape
    n = t // CHUNK
    rs = lambda a: a.reshape(b, h, n, CHUNK, a.shape[-1])
    q, k, v, log_f = rs(q), rs(k), rs(v), rs(log_f)
    g = jnp.cumsum(log_f, axis=3)
    tri = jnp.tril(jnp.ones((CHUNK, CHUNK), bool))[:, :, None]
    diff = g[:, :, :, :, None, :] - g[:, :, :, None, :, :]
    decay = jnp.where(tri, jnp.exp(jnp.where(tri, diff, 0.0)), 0.0)
    attn = jnp.einsum('bhntd,bhnsd,bhntsd->bhnts', q, k, decay)
    o_intra = jnp.einsum('bhnts,bhnsv->bhntv', attn, v)
    g_last = g[:, :, :, -1:, :]
    u = jnp.einsum('bhnsd,bhnsv->bhndv', k * jnp.exp(g_last - g), v)
    a_chunk = jnp.exp(g_last[:, :, :, 0])

    def step(s, inp):
        a_n, u_n = inp
        return a_n[..., None] * s + u_n, s

    s_final, s_before = lax.scan(step, s0, (jnp.moveaxis(a_chunk, 2, 0), jnp.moveaxis(u, 2, 0)))
    s_before = jnp.moveaxis(s_before, 0, 2)
    o_inter = jnp.einsum('bhntd,bhndv->bhntv', q * jnp.exp(g), s_before)
    return (o_intra + o_inter).reshape(b, h, t, -1), s_final


def _gated_delta_chunked(q, k, v, glog, beta, s0):
    b, h, t, _ = q.shape
    n = t // CHUNK
    rs = lambda a: a.reshape(b, h, n, CHUNK, a.shape[-1])
    q, k, v = rs(q), rs(k), rs(v)
    g = jnp.cumsum(glog.reshape(b, h, n, CHUNK), axis=-1)
    beta = beta.reshape(b, h, n, CHUNK)
    tri = jnp.tril(jnp.ones((CHUNK, CHUNK), bool))
    strict = jnp.tril(jnp.ones((CHUNK, CHUNK), bool), -1)
    diff = g[..., :, None] - g[..., None, :]
    L = jnp.where(tri, jnp.exp(jnp.where(tri, diff, 0.0)), 0.0)
    kb = k * beta[..., None]
    A = jnp.where(strict, jnp.einsum('bhntd,bhnsd->bhnts', kb, k) * L, 0.0)
    T = jnp.eye(CHUNK, dtype=A.dtype) + A
    u = lax.linalg.triangular_solve(T, v * beta[..., None], left_side=True, lower=True, unit_diagonal=True)
    w = lax.linalg.triangular_solve(T, kb * jnp.exp(g)[..., None], left_side=True, lower=True, unit_diagonal=True)
    attn = jnp.where(tri, jnp.einsum('bhntd,bhnsd->bhnts', q, k) * L, 0.0)
    qg = q * jnp.exp(g)[..., None]
    g_last = g[..., -1]
    kdec = k * jnp.exp(g_last[..., None] - g)[..., None]

    def step(s, inp):
        u_n, w_n, attn_n, qg_n, kdec_n, gl_n = inp
        v_new = u_n - jnp.einsum('bhtd,bhdv->bhtv', w_n, s)
        o_n = jnp.einsum('bhtd,bhdv->bhtv', qg_n, s) + jnp.einsum('bhts,bhsv->bhtv', attn_n, v_new)
        s = jnp.exp(gl_n)[..., None, None] * s + jnp.einsum('bhsd,bhsv->bhdv', kdec_n, v_new)
        return s, o_n

    xs = tuple(jnp.moveaxis(a, 2, 0) for a in (u, w, attn, qg, kdec, g_last))
    s_final, o = lax.scan(step, s0, xs)
    return jnp.moveaxis(o, 0, 2).reshape(b, h, t, -1), s_final


def _mixer_ab(h, w_in, w_out, lb, conv_w, a_log, dt_bias, gn_a, gn_b, s_hgrn, s_gdn):
    f32 = jnp.float32
    q_a, f_fw, f_bw, i_a, g_a, qkv_b, g_b, a_fw, a_bw, b_fw, b_bw = _split(h @ w_in, _AB_SIZES)
    lbh = lb.reshape(H_A, 1, DK_A)
    qa = _heads(jax.nn.silu(q_a), H_A).astype(f32) * DK_A ** -0.5
    va = _heads(i_a, H_A).astype(f32)
    s_h = s_hgrn.astype(f32)

    def hgrn_dir(f_raw, s0, flip):
        f = lbh + (1.0 - lbh) * jax.nn.sigmoid(_heads(f_raw, H_A).astype(f32))
        args = [qa, 1.0 - f, va, jnp.log(f)]
        if flip:
            args = [jnp.flip(a, axis=2) for a in args]
        o, s = _gla_chunked(*args, s0)
        return (jnp.flip(o, axis=2) if flip else o), s

    oa_f, sa_f = hgrn_dir(f_fw, s_h[:, 0], False)
    oa_b, sa_b = hgrn_dir(f_bw, s_h[:, 1], True)
    o_a = _rmsnorm(oa_f + oa_b, gn_a) * jax.nn.silu(_heads(g_a, H_A).astype(f32))
    qkv = jax.nn.silu(_short_conv(qkv_b, conv_w))
    q_b, k_b, v_b = _split(qkv, (H_B * DK_B, H_B * DK_B, H_B * DV_B))
    qb = _l2norm(_heads(q_b, H_B).astype(f32)) * DK_B ** -0.5
    kb = _l2norm(_heads(k_b, H_B).astype(f32))
    vb = _heads(v_b, H_B).astype(f32)
    s_g = s_gdn.astype(f32)

    def gdn_dir(a_raw, b_raw, d, s0, flip):
        beta = jax.nn.sigmoid(b_raw.astype(f32)).transpose(0, 2, 1)
        glog = (-jnp.exp(a_log[d].astype(f32))
                * jax.nn.softplus(a_raw.astype(f32) + dt_bias[d].astype(f32))).transpose(0, 2, 1)
        args = [qb, kb, vb, glog, beta]
        if flip:
            args = [jnp.flip(a, axis=2) for a in args]
        o, s = _gated_delta_chunked(*args, s0)
        return (jnp.flip(o, axis=2) if flip else o), s

    ob_f, sb_f = gdn_dir(a_fw, b_fw, 0, s_g[:, 0], False)
    ob_b, sb_b = gdn_dir(a_bw, b_bw, 1, s_g[:, 1], True)
    o_b = _rmsnorm(ob_f + ob_b, gn_b) * jax.nn.silu(_heads(g_b, H_B).astype(f32))
    o = jnp.concatenate([_merge(o_a), _merge(o_b)], axis=-1).astype(h.dtype)
    return (o @ w_out,
            jnp.stack([sa_f, sa_b], axis=1).astype(h.dtype),
            jnp.stack([sb_f, sb_b], axis=1).astype(h.dtype))


def _na_qkv(h, w_qkv, qn, kn):
    q, k, v = jnp.split(h @ w_qkv, 3, axis=-1)
    return _rmsnorm(_heads(q, H_C), qn), _rmsnorm(_heads(k, H_C), kn), _heads(v, H_C)


def _ctx_attention(q, k, v):
    b, h, t, d = q.shape
    qb = jnp.moveaxis(q.reshape(b, h, t // Q_BLOCK, Q_BLOCK, d), 2, 0)

    def blk(qi):
        s = jnp.einsum('bhqd,bhkd->bhqk', qi, k).astype(jnp.float32) * HD_C ** -0.5
        p = jax.nn.softmax(s, axis=-1)
        return jnp.einsum('bhqk,bhkd->bhqd', p.astype(v.dtype), v)

    o = lax.map(blk, qb)
    return jnp.moveaxis(o, 0, 2).reshape(b, h, t, d)


def _na_latent(q, k, v, k_ctx, v_ctx, rpb):
    b, h, t, d = q.shape
    rows = t // GRID_W
    kh = min(KH_MAX, rows)
    ncb = GRID_W // QBW
    qg = q.reshape(b, h, rows, GRID_W, d)
    kg = k.reshape(b, h, rows, GRID_W, d)
    vg = v.reshape(b, h, rows, GRID_W, d)
    qc = np.arange(GRID_W).reshape(ncb, QBW)
    kc0 = np.clip(np.arange(ncb) * QBW - KW // 2, 0, GRID_W - KBW)
    kcols = kc0[:, None] + np.arange(KBW)
    cstart = np.clip(qc - KW // 2, 0, GRID_W - KW)
    col_valid = (kcols[:, None, :] >= cstart[..., None]) & (kcols[:, None, :] < cstart[..., None] + KW)
    col_idx = np.clip(kcols[:, None, :] - qc[:, :, None] + KW - 1, 0, 2 * KW - 2)
    rpb_c = rpb.astype(jnp.float32)[:, :, col_idx]
    scale = HD_C ** -0.5

    def row_block(r):
        rs = jnp.clip(r - kh // 2, 0, rows - kh)
        q_r = lax.dynamic_index_in_dim(qg, r, axis=2, keepdims=False).reshape(b, h, ncb, QBW, d)
        k_r = lax.dynamic_slice_in_dim(kg, rs, kh, axis=2)[:, :, :, kcols]
        v_r = lax.dynamic_slice_in_dim(vg, rs, kh, axis=2)[:, :, :, kcols]
        s_win = jnp.einsum('bhjqd,bhajkd->bhjqak', q_r, k_r).astype(jnp.float32) * scale
        dr_idx = rs + jnp.arange(kh) - r + (KH_MAX - 1)
        bias = jnp.transpose(rpb_c[:, dr_idx], (0, 2, 3, 1, 4))
        s_win = jnp.where(col_valid[:, :, None, :], s_win + bias, NEG_INF).reshape(b, h, ncb, QBW, kh * KBW)
        s_ctx = jnp.einsum('bhjqd,bhld->bhjql', q_r, k_ctx).astype(jnp.float32) * scale
        p = jax.nn.softmax(jnp.concatenate([s_win, s_ctx], axis=-1), axis=-1)
        p_win = p[..., :kh * KBW].reshape(b, h, ncb, QBW, kh, KBW).astype(v.dtype)
        p_ctx = p[..., kh * KBW:].astype(v.dtype)
        o = (jnp.einsum('bhjqak,bhajkd->bhjqd', p_win, v_r)
             + jnp.einsum('bhjql,bhld->bhjqd', p_ctx, v_ctx.astype(v.dtype)))
        return o.reshape(b, h, GRID_W, d)

    o = lax.map(row_block, jnp.arange(rows))
    return jnp.moveaxis(o, 0, 2).reshape(b, h, t, d)


def _mlp(h, w1, w2):
    return jnp.square(jax.nn.relu(h @ w1)) @ w2


def setup_inputs(seed: int = 0) -> dict:
    key = jax.random.key(seed)
    ks = jax.random.split(key, 32)
    nrm = lambda k, shape, s: jax.random.normal(k, shape, jnp.float32) * s
    d = D_MODEL
    return {
        'x_prompt': nrm(ks[0], (BATCH, SEQ, d), 1.0),
        'x_sample': nrm(ks[1], (DEC_BATCH, DEC_SEQ, d), 1.0),
        'state_hgrn': nrm(ks[2], (DEC_BATCH, N_EVEN, 2, H_A, DK_A, DV_A), 0.5),
        'state_gdn': nrm(ks[3], (DEC_BATCH, N_EVEN, 2, H_B, DK_B, DV_B), 0.1),
        'cache_na_k': nrm(ks[4], (DEC_BATCH, N_ODD, H_C, PAST_LEN, HD_C), 1.0),
        'cache_na_v': nrm(ks[5], (DEC_BATCH, N_ODD, H_C, PAST_LEN, HD_C), 1.0),
        'c': nrm(ks[6], (DEC_BATCH, d), 1.0),
        'c_ctx': nrm(ks[7], (d,), 1.0),
        'ada_w': nrm(ks[8], (DEPTH, d, 6 * d), 0.5 * d ** -0.5),
        'ada_b': nrm(ks[9], (DEPTH, 6 * d), 0.02),
        'norm_g': 1.0 + nrm(ks[10], (DEPTH, 2, d), 0.01),
        'w_in_ab': nrm(ks[11], (N_EVEN, d, D_IN_AB), d ** -0.5),
        'w_out_ab': nrm(ks[12], (N_EVEN, D_MIX_AB, d), D_MIX_AB ** -0.5),
        'hgrn_lb': nrm(ks[13], (DEPTH + 1, H_A * DK_A), 0.1),
        'gdn_conv': nrm(ks[14], (N_EVEN, SHORT_CONV, 1, C_CONV), SHORT_CONV ** -0.5),
        'gdn_a_log': jnp.log(jax.random.uniform(ks[15], (N_EVEN, 2, H_B), jnp.float32, 1.0, 8.0)),
        'gdn_dt_bias': nrm(ks[16], (N_EVEN, 2, H_B), 0.1),
        'gn_hgrn': 1.0 + nrm(ks[17], (N_EVEN, DV_A), 0.01),
        'gn_gdn': 1.0 + nrm(ks[18], (N_EVEN, DV_B), 0.01),
        'w_qkv_na': nrm(ks[19], (N_ODD, d, 3 * H_C * HD_C), d ** -0.5),
        'qn_na': 1.0 + nrm(ks[20], (N_ODD, HD_C), 0.01),
        'kn_na': 1.0 + nrm(ks[21], (N_ODD, HD_C), 0.01),
        'rpb_na': nrm(ks[22], (N_ODD, H_C, 2 * KH_MAX - 1, 2 * KW - 1), 0.02),
        'w_out_na': nrm(ks[23], (N_ODD, H_C * HD_C, d), (H_C * HD_C) ** -0.5),
        'w_mlp1': nrm(ks[24], (DEPTH, d, D_FF), d ** -0.5),
        'w_mlp2': nrm(ks[25], (DEPTH, D_FF, d), D_FF ** -0.5),
    }


def reference(x_prompt, x_sample, state_hgrn, state_gdn, cache_na_k, cache_na_v, c,
              c_ctx, ada_w, ada_b, norm_g, w_in_ab, w_out_ab, hgrn_lb, gdn_conv,
              gdn_a_log, gdn_dt_bias, gn_hgrn, gn_gdn, w_qkv_na, qn_na, kn_na, rpb_na,
              w_out_na, w_mlp1, w_mlp2):
    xp, xs = x_prompt, x_sample
    bp = xp.shape[0]
    lbs = jnp.cumsum(jax.nn.softmax(hgrn_lb.astype(jnp.float32), axis=0), axis=0)
    new_hgrn, new_gdn, new_k, new_v = [], [], [], []
    for l in range(DEPTH):
        p_sh1, p_sc1, p_g1, p_sh2, p_sc2, p_g2 = _modulation(c_ctx[None, :], ada_w[l], ada_b[l])
        s_sh1, s_sc1, s_g1, s_sh2, s_sc2, s_g2 = _modulation(c, ada_w[l], ada_b[l])
        hp = _rmsnorm(xp, norm_g[l, 0]) * (1.0 + p_sc1) + p_sh1
        hs = _rmsnorm(xs, norm_g[l, 0]) * (1.0 + s_sc1) + s_sh1
        if l % 2 == 0:
            e = l // 2
            w = (w_in_ab[e], w_out_ab[e], lbs[l], gdn_conv[e], gdn_a_log[e], gdn_dt_bias[e], gn_hgrn[e], gn_gdn[e])
            zh = jnp.zeros((bp, 2, H_A, DK_A, DV_A), jnp.float32)
            zg = jnp.zeros((bp, 2, H_B, DK_B, DV_B), jnp.float32)
            mp, sh_new, sg_new = _mixer_ab(hp, *w, zh, zg)
            ms, _, _ = _mixer_ab(hs, *w, state_hgrn[:, e], state_gdn[:, e])
            new_hgrn.append(sh_new)
            new_gdn.append(sg_new)
        else:
            o = l // 2
            qp, kp, vp = _na_qkv(hp, w_qkv_na[o], qn_na[o], kn_na[o])
            mp = _merge(_ctx_attention(qp, kp, vp)) @ w_out_na[o]
            qs, ks_, vs = _na_qkv(hs, w_qkv_na[o], qn_na[o], kn_na[o])
            ms = _merge(_na_latent(qs, ks_, vs, cache_na_k[:, o], cache_na_v[:, o], rpb_na[o])) @ w_out_na[o]
            new_k.append(kp)
            new_v.append(vp)
        xp = xp + p_g1 * mp
        xs = xs + s_g1 * ms
        hp = _rmsnorm(xp, norm_g[l, 1]) * (1.0 + p_sc2) + p_sh2
        hs = _rmsnorm(xs, norm_g[l, 1]) * (1.0 + s_sc2) + s_sh2
        xp = xp + p_g2 * _mlp(hp, w_mlp1[l], w_mlp2[l])
        xs = xs + s_g2 * _mlp(hs, w_mlp1[l], w_mlp2[l])
    return (xp, xs, jnp.stack(new_hgrn, axis=1), jnp.stack(new_gdn, axis=1),
            jnp.stack(new_k, axis=1), jnp.stack(new_v, axis=1))
```

```python
import os
import numpy as np
from contextlib import ExitStack
import concourse.bass as bass
import concourse.mybir as mybir
from concourse.bass_utils import run_bass_kernel_spmd

F32 = mybir.dt.float32
F32R = mybir.dt.float32r
BF16 = mybir.dt.bfloat16
FAST_MLP = 1
FAST_PROJ = 1
ALU = mybir.AluOpType
AF = mybir.ActivationFunctionType
AX = mybir.AxisListType

NCORES = 8
D = 1024
T = 1024
NSEG = 4
DFF = 4096
DIN_AB = 4624
EPS = 1e-6

ENGS = ("pe", "act", "dve", "pool", "sp")


class Buf:
    __slots__ = ("name", "parent", "kids", "w", "r")

    def __init__(self, name, parent=None):
        self.name = name
        self.parent = parent
        self.kids = {}
        self.w = None
        self.r = {}

    def sub(self, key):
        k = self.kids.get(key)
        if k is None:
            k = Buf(f"{self.name}.{key}", self)
            self.kids[key] = k
        return k

    def _related(self):
        out = [self]
        p = self.parent
        while p is not None:
            out.append(p)
            p = p.parent
        stack = list(self.kids.values())
        while stack:
            n = stack.pop()
            out.append(n)
            stack.extend(n.kids.values())
        return out


class Sched:
    NDMA = 48

    def __init__(self, nc, stack):
        self.nc = nc
        self.streams = {e: [] for e in ENGS}
        self.sems = {}
        for e in ("pe", "act", "dve", "pool"):
            self.sems[e] = stack.enter_context(nc.semaphore(f"s_{e}"))
        self.cnt = {e: 0 for e in ("pe", "act", "dve", "pool")}
        self.dma_sems = [stack.enter_context(nc.semaphore(f"s_dma{i}")) for i in range(self.NDMA)]
        for i, s in enumerate(self.dma_sems):
            self.sems[("dma", i)] = s
        self.dma_val = [0] * self.NDMA
        self.dma_i = 0
        self.known = {e: {} for e in ENGS}
        self.n_ops = 0
        self.n_waits = 0

    def _deps(self, reads, writes):
        deps = {}

        def add(sv):
            if sv is None:
                return
            k, v = sv
            if deps.get(k, 0) < v:
                deps[k] = v

        for b in reads:
            for n in b._related():
                add(n.w)
        for b in writes:
            for n in b._related():
                add(n.w)
                for k, v in n.r.items():
                    add((k, v))
        return deps

    def _emit_waits(self, eng, deps):
        kn = self.known[eng]
        for k, v in deps.items():
            if eng == "pe" and k == "pe":
                continue
            if kn.get(k, 0) >= v:
                continue
            kn[k] = v
            self.streams[eng].append(("wait", k, v))
            self.n_waits += 1

    def _record(self, key, val, reads, writes):
        for b in reads:
            if b.r.get(key, 0) < val:
                b.r[key] = val
        for b in writes:
            b.w = (key, val)
            b.r = {}

    def op(self, eng, meth, reads=(), writes=(), **kw):
        fn = (meth, kw)
        deps = self._deps(reads, writes)
        for b in reads:
            if b.name.startswith("ps"):
                for k2, v2 in b.r.items():
                    if k2 != eng and max(deps.get(k2, 0), self.known[eng].get(k2, 0)) < v2:
                        raise AssertionError(f"PSUM bank {b.name} may be read concurrently by {k2} and {eng}")
        self._emit_waits(eng, deps)
        self.cnt[eng] += 1
        val = self.cnt[eng]
        self.streams[eng].append(("op", fn, eng, 1))
        self._record(eng, val, reads, writes)
        self.n_ops += 1

    def dma(self, reads=(), writes=(), queue="sp", **kw):
        fn = ("dma_start", kw)
        deps = self._deps(reads, writes)
        i = self.dma_i % self.NDMA
        self.dma_i += 1
        key = ("dma", i)
        if self.dma_val[i] > 0 and deps.get(key, 0) < self.dma_val[i]:
            deps[key] = self.dma_val[i]
        self._emit_waits(queue, deps)
        self.dma_val[i] += 16
        val = self.dma_val[i]
        self.streams[queue].append(("op", fn, key, 16))
        self._record(key, val, reads, writes)
        self.n_ops += 1

    def all_deps(self):
        deps = {}
        for i in range(self.NDMA):
            if self.dma_val[i] > 0:
                deps[("dma", i)] = self.dma_val[i]
        for e in ("pe", "act", "dve", "pool"):
            if self.cnt[e] > 0:
                deps[e] = self.cnt[e]
        return deps

    def barrier(self):
        deps = self.all_deps()
        for e in ENGS:
            self._emit_waits(e, dict(deps))

    def finish(self):
        self._emit_waits("sp", self.all_deps())

    def emit(self):
        nc = self.nc
        streams = self.streams
        sems = self.sems

        def run(stream, e):
            for it in stream:
                if it[0] == "wait":
                    e.wait_ge(sems[it[1]], it[2])
                else:
                    _, fn, key, inc = it
                    getattr(e, fn[0])(**fn[1]).then_inc(sems[key], inc)

        with nc.Block() as block:
            @block.sync
            def _(e):
                run(streams["sp"], e)

            @block.tensor
            def _(e):
                run(streams["pe"], e)

            @block.scalar
            def _(e):
                run(streams["act"], e)

            @block.vector
            def _(e):
                run(streams["dve"], e)

            @block.gpsimd
            def _(e):
                run(streams["pool"], e)


AW = 48128
OFF_XT = 0
OFF_HT = 8192
OFF_OA = 16384
OFF_WA = 24576
OFF_G = 32768
G_WORDS = AW - OFF_G


class K:
    def __init__(self, debug=None):
        self.debug = debug or []
        self.nc = bass.Bass("TRN2", target_bir_lowering=False)
        self.stack = ExitStack()
        self.S = Sched(self.nc, self.stack)
        self.din = {}
        self.dout = {}
        self.dbg_out = {}
        nc = self.nc
        self.arena = self.stack.enter_context(nc.sbuf_tensor("arena", [128, AW], F32))
        self.small = self.stack.enter_context(nc.sbuf_tensor("small", [128, 2560], F32))
        self.small_off = 0
        self.cst = self.stack.enter_context(nc.sbuf_tensor("cst", [128, 1024], F32))
        self.cst2 = self.stack.enter_context(nc.sbuf_tensor("cst2", [128, 256], F32))
        self.cst3 = self.stack.enter_context(nc.sbuf_tensor("cst3", [128, 256], F32))
        self.psb2 = [self.stack.enter_context(nc.psum_tensor(f"psb{i}", [128, 1024], F32)) for i in range(4)]
        self.psb = [self.psb2[i // 2][:, (i % 2) * 512:(i % 2) * 512 + 512] for i in range(8)]
        self.pair_rr = 0
        self.PS = [Buf(f"ps{i}") for i in range(8)]
        self.B = {}
        self.ps_rr = 0
        self.acc_rr = 0
        self.wa_rr = 0

    def inp(self, name, shape):
        t = self.nc.dram_tensor(name, list(shape), F32, kind="ExternalInput")
        self.din[name] = t
        return t.ap()

    def outp(self, name, shape):
        t = self.nc.dram_tensor(name, list(shape), F32, kind="ExternalOutput")
        self.dout[name] = t
        return t.ap()

    def buf(self, name):
        b = self.B.get(name)
        if b is None:
            b = Buf(name)
            self.B[name] = b
        return b

    def ar(self, off, n):
        return self.arena[:, off:off + n]

    def sm(self, n):
        o = self.small_off
        self.small_off += n
        assert self.small_off <= 2560
        return self.small[:, o:o + n]

    def dump(self, name, ap, shape, reads):
        if name not in self.debug:
            return
        o = self.outp("dbg_" + name, shape)
        self.S.dma(out=o, in_=ap, reads=reads, queue="act")

    def close(self):
        self.S.finish()
        self.S.emit()
        self.stack.close()


def build(debug=None, stages=("all",)):
    k = K(debug)
    nc, S = k.nc, k.S
    x_in = k.inp("x", [T, D])
    normg = k.inp("normg", [128, 32])
    consts = k.inp("consts", [128, 1024])
    y_out = k.outp("y", [T, D])
    fake_mod = "fakemod" in stages
    if fake_mod:
        modT_in = k.inp("modT_in", [128, 96])
    else:
        condc = k.inp("condc", [128, 8])
        ada_w = k.inp("ada_w", [2, D, 6 * D])
        ada_bc = k.inp("ada_bc", [128, 2 * 48])
    if "mlp" in stages or "all" in stages:
        w_mlp1 = k.inp("w_mlp1", [2, D, DFF])
        w_mlp2 = k.inp("w_mlp2", [2, DFF, D])

    CST = k.buf("cst")
    S.dma(out=k.cst[:], in_=consts, writes=[CST])
    consts2 = k.inp("consts2", [128, 256])
    S.dma(out=k.cst2[:], in_=consts2, writes=[CST])
    consts3 = k.inp("consts3", [128, 256])
    S.dma(out=k.cst3[:], in_=consts3, writes=[CST])
    ident = k.cst[:, 0:128]
    ones_d = k.cst[:, 128:256]
    ones1 = k.cst[:, 256:384]
    epsc = k.cst[:, 384:385]

    XT = k.ar(OFF_XT, 8192).rearrange("p (k t) -> p k t", k=8)
    HT = k.ar(OFF_HT, 8192).rearrange("p (k t) -> p k t", k=8)
    OA = k.ar(OFF_OA, 8192).rearrange("p (k t) -> p k t", k=8)
    bXT, bHT, bOA = k.buf("XT"), k.buf("HT"), k.buf("OA")
    WA = [k.ar(OFF_WA + i * 4096, 4096).rearrange("p (k n) -> p k n", k=8) for i in range(2)]
    bWA = [k.buf(f"WA{i}") for i in range(2)]

    def next_ps():
        i = k.ps_rr % 4
        k.ps_rr += 1
        return k.psb[i], k.PS[i]

    def bfv(ap):
        return ap.bitcast(BF16)[:, 0:ap.shape[-1]] if FAST_PROJ else ap

    def HTr(kk, half):
        return bfv(HT[:, kk, half * 512:(half + 1) * 512])

    def HTt(kk, tt):
        return HTr(kk, tt // 4)[:, (tt % 4) * 128:(tt % 4 + 1) * 128]

    def cast_rows(t3, bt, nk, ncols):
        if not FAST_PROJ:
            return
        for kk in range(nk):
            S.op("act" if kk % 2 == 0 else "pool", "copy" if kk % 2 == 0 else "tensor_copy", out=bfv(t3[:, kk, 0:ncols]), in_=t3[:, kk, 0:ncols],
                 reads=[bt], writes=[bt])

    def wsub(bt, j):
        return bt if FAST_PROJ else bt.sub(j)

    def wv(t3, kk, c0, c1, ncols):
        return bfv(t3[:, kk, 0:ncols])[:, c0:c1]

    evac_rr = [0]

    def evac(out_ap, in_ap, reads, writes):
        i = evac_rr[0]
        evac_rr[0] += 1
        if i % 2 == 0:
            S.op("act", "copy", out=out_ap, in_=in_ap, reads=reads, writes=writes)
        else:
            S.op("dve", "tensor_copy", out=out_ap, in_=in_ap, reads=reads, writes=writes)

    def mm(out, lhsT, rhs, start, stop, reads, writes):
        S.op("pe", "matmul", out=out, lhsT=lhsT, rhs=rhs, start=start, stop=stop, reads=reads, writes=writes)

    def tr(out, in_, reads, writes):
        n = in_.shape[0]
        S.op("pe", "transpose", out=out, in_=in_, identity=k.cst[0:n, 0:n], reads=list(reads) + [CST], writes=writes)

    STG = [k.ar(OFF_G + i * 1024, 1024) for i in range(2)]
    bSTG = [k.buf(f"stg{i}") for i in range(2)]
    for tt in range(8):
        st, bst = STG[tt % 2], bSTG[tt % 2]
        S.dma(out=st, in_=x_in[tt * 128:(tt + 1) * 128, :], writes=[bst])
        for kq in range(2):
            ps, bps = next_ps()
            for j in range(4):
                kk = kq * 4 + j
                tr(ps[:, j * 128:(j + 1) * 128], st[:, kk * 128:(kk + 1) * 128], [bst], [bps])
            evac(XT[:, kq * 4:kq * 4 + 4, tt * 128:(tt + 1) * 128], ps[:].rearrange("p (j t) -> p j t", j=4),
                 reads=[bps], writes=[bXT.sub(tt // 4)])

    modT = k.sm(96)
    bmod = k.buf("modT")
    ng = k.sm(32)
    bng = k.buf("ng")
    S.dma(out=ng, in_=normg, writes=[bng], queue="act")

    def load_w(src_ap):
        i = k.wa_rr % 2
        k.wa_rr += 1
        S.dma(out=WA[i], in_=src_ap, writes=[bWA[i]])
        return WA[i], bWA[i]

    if not fake_mod:
        scond = k.sm(8)
        bsc = k.buf("scond")
        S.dma(out=scond, in_=condc, writes=[bsc], queue="act")
        S.op("act", "activation", out=scond, in_=scond, func=AF.Silu, reads=[bsc], writes=[bsc])
        adab = k.sm(96)
        badab = k.buf("adab")
        S.dma(out=adab, in_=ada_bc, writes=[badab], queue="act")
        rowb = [k.sm(512)[0:1, :] for _ in range(2)]
        brow = [k.buf(f"row{i}") for i in range(2)]
        blk = 0
        for l in range(2):
            for nb in range(12):
                wt, bwt = load_w(ada_w[l, :, nb * 512:(nb + 1) * 512].rearrange("(k p) n -> p k n", p=128))
                ps, bps = next_ps()
                for kk in range(8):
                    mm(ps[0:1, :], scond[:, kk:kk + 1], wt[:, kk, :], kk == 0, kk == 7, [bsc, bwt], [bps])
                rb, brb = rowb[blk % 2], brow[blk % 2]
                blk += 1
                evac(rb, ps[0:1, :], reads=[bps], writes=[brb])
                ps2, bps2 = next_ps()
                for j in range(4):
                    mm(ps2[:, j:j + 1], rb[0:1, j * 128:(j + 1) * 128], ones1[0:1, 0:1], True, True, [brb, CST], [bps2])
                c0 = l * 48 + nb * 4
                S.op("dve", "tensor_tensor", out=modT[:, c0:c0 + 4], in0=ps2[:, 0:4], in1=adab[:, c0:c0 + 4], op=ALU.add,
                     reads=[bps2, badab], writes=[bmod.sub(c0)])
    else:
        S.dma(out=modT, in_=modT_in, writes=[bmod], queue="act")
    k.dump("modT", modT, [128, 96], [bmod])

    def mcol(l, part, kk):
        c = l * 48 + part * 8 + kk
        return modT[:, c:c + 1]

    gsc = k.sm(32)
    bgsc = k.buf("gsc")
    for l in range(2):
        for j in range(2):
            c0 = l * 48 + (1 + 3 * j) * 8
            o0 = (l * 2 + j) * 8
            S.op("dve", "scalar_tensor_tensor", out=gsc[:, o0:o0 + 8], in0=modT[:, c0:c0 + 8], scalar=1.0,
                 in1=ng[:, o0:o0 + 8], op0=ALU.add, op1=ALU.mult, reads=[bmod, bng], writes=[bgsc])

    SQ = [k.ar(OFF_G + 2048 + i * 512, 512) for i in range(3)]
    bSQ = [k.buf(f"sq{i}") for i in range(3)]
    RS = k.ar(OFF_G + 2048 + 3 * 512, 512)
    bRS = k.buf("rs")
    sq_rr = [0]

    def norm_mod(l, j):
        for half in range(2):
            tsl = slice(half * 512, (half + 1) * 512)
            ps, bps = next_ps()
            for kk in range(8):
                i = sq_rr[0] % 3
                sq_rr[0] += 1
                S.op("act", "activation", out=SQ[i], in_=XT[:, kk, tsl], func=AF.Square, reads=[bXT], writes=[bSQ[i]])
                mm(ps[:, :], ones_d, SQ[i], kk == 0, kk == 7, [bSQ[i], CST], [bps])
            S.op("act", "activation", out=RS, in_=ps[:, :], func=AF.Sqrt, bias=epsc, scale=1.0, reads=[bps, CST], writes=[bRS])
            S.op("dve", "reciprocal", out=RS, in_=RS, reads=[bRS], writes=[bRS])
            for kk in range(8):
                o0 = (l * 2 + j) * 8 + kk
                hb = bHT.sub((kk, half))
                S.op("dve", "scalar_tensor_tensor", out=HT[:, kk, tsl], in0=XT[:, kk, tsl], scalar=gsc[:, o0:o0 + 1],
                     in1=RS, op0=ALU.mult, op1=ALU.mult, reads=[bXT, bgsc, bRS], writes=[hb])
                ho = HT[:, kk, tsl].bitcast(BF16)[:, 0:512] if ((FAST_MLP and j == 1) or (FAST_PROJ and j == 0)) else HT[:, kk, tsl]
                S.op("act", "activation", out=ho, in_=HT[:, kk, tsl], func=AF.Identity, bias=mcol(l, 3 * j, kk), scale=1.0,
                     reads=[hb, bmod], writes=[hb])

    HID = [k.ar(OFF_OA + i * 4096, 4096).rearrange("p (c t) -> p c t", c=4) for i in range(2)]
    bHID = [k.buf(f"hid{i}") for i in range(2)]
    W2 = [k.ar(OFF_G + 4096 + i * 4096, 4096).rearrange("p (c n) -> p c n", c=4) for i in range(2)]
    bW2 = [k.buf(f"W2_{i}") for i in range(2)]

    def mlp(l):
        def rnd(ap):
            if not FAST_MLP:
                return ap
            n = ap.shape[-1]
            return ap.bitcast(BF16)[:, 0:n]

        def round_tile(t3, bt, nk):
            if not FAST_MLP:
                return
            for kk in range(nk):
                if kk % 2 == 0:
                    S.op("act", "copy", out=rnd(t3[:, kk, :]), in_=t3[:, kk, :], reads=[bt], writes=[bt])
                else:
                    S.op("pool", "tensor_copy", out=rnd(t3[:, kk, :]), in_=t3[:, kk, :], reads=[bt], writes=[bt])

        for jb in range(8):
            wt, bwt = load_w(w_mlp1[l, :, jb * 512:(jb + 1) * 512].rearrange("(k p) n -> p k n", p=128))
            round_tile(wt, bwt, 8)
            w2, bw2 = W2[jb % 2], bW2[jb % 2]
            S.dma(out=w2, in_=w_mlp2[l, jb * 512:(jb + 1) * 512, :].rearrange("(c p) n -> p c n", p=128), writes=[bw2])
            round_tile(w2, bw2, 4)
            hid, bhid = HID[jb % 2], bHID[jb % 2]
            for c in range(4):
                for half in range(2):
                    tsl = slice(half * 512, (half + 1) * 512)
                    ps, bps = next_ps()
                    for kk in range(8):
                        mm(ps[:, :], rnd(wt[:, kk, :])[:, c * 128:(c + 1) * 128], rnd(HT[:, kk, tsl]), kk == 0, kk == 7, [bwt, bHT], [bps])
                    hb = bhid.sub((c, half))
                    S.op("act", "activation", out=hid[:, c, tsl], in_=ps[:, :], func=AF.Relu, reads=[bps], writes=[hb])
                    S.op("pool", "tensor_tensor", out=rnd(hid[:, c, tsl]), in0=hid[:, c, tsl], in1=hid[:, c, tsl], op=ALU.mult,
                         reads=[hb], writes=[hb])
            for n in range(8):
                for half in range(2):
                    tsl = slice(half * 512, (half + 1) * 512)
                    ps, bps = next_ps()
                    for c in range(4):
                        mm(ps[:, :], rnd(w2[:, c, :])[:, n * 128:(n + 1) * 128], rnd(hid[:, c, tsl]), c == 0, c == 3, [bw2, bhid], [bps])
                    S.op("dve", "scalar_tensor_tensor", out=XT[:, n, tsl], in0=ps[:, :], scalar=mcol(l, 5, n), in1=XT[:, n, tsl],
                         op0=ALU.mult, op1=ALU.add, reads=[bps, bmod], writes=[bXT.sub(half)])

    bd64 = k.cst[:, 512:640]
    GA = OFF_G + 4096

    def next_pair():
        j = k.pair_rr % 2
        k.pair_rr += 1
        return k.psb2[j], k.PS[2 * j], k.PS[2 * j + 1]

    def next_acc():
        j = 2 + (k.acc_rr % 2)
        k.acc_rr += 1
        return k.psb2[j], k.PS[2 * j].parent_pair if False else (k.PS[2 * j], k.PS[2 * j + 1])

    def out_proj(w_dram, l):
        for nb in range(2):
            wt, bwt = load_w(w_dram[:, nb * 512:(nb + 1) * 512].rearrange("(k p) n -> p k n", p=128))
            cast_rows(wt, bwt, 8, 512)
            if FAST_PROJ and nb == 0:
                for kk in range(8):
                    for half in range(2):
                        tsl = slice(half * 512, (half + 1) * 512)
                        S.op("act" if (kk + half) % 2 == 0 else "pool", "copy" if (kk + half) % 2 == 0 else "tensor_copy",
                             out=bfv(OA[:, kk, tsl]), in_=OA[:, kk, tsl], reads=[bOA], writes=[bOA])
            for j in range(4):
                n = nb * 4 + j
                for half in range(2):
                    tsl = slice(half * 512, (half + 1) * 512)
                    ps, bps = next_ps()
                    for kk in range(8):
                        mm(ps[:, :], wv(wt, kk, j * 128, (j + 1) * 128, 512), bfv(OA[:, kk, tsl]), kk == 0, kk == 7, [bwt, bOA], [bps])
                    S.op("dve", "scalar_tensor_tensor", out=XT[:, n, tsl], in0=ps[:, :], scalar=mcol(l, 2, n), in1=XT[:, n, tsl],
                         op0=ALU.mult, op1=ALU.add, reads=[bps, bmod], writes=[bXT.sub(half)])

    def head_norm(X, bX, gcol, bg):
        for half in range(2):
            tsl = slice(half * 512, (half + 1) * 512)
            i = sq_rr[0] % 3
            sq_rr[0] += 1
            S.op("act", "activation", out=SQ[i], in_=X[:, tsl], func=AF.Square, reads=[bX], writes=[bSQ[i]])
            ps, bps = next_ps()
            mm(ps[:, :], bd64, SQ[i], True, True, [bSQ[i], CST], [bps])
            S.op("act", "activation", out=RS, in_=ps[:, :], func=AF.Sqrt, bias=epsc, scale=1.0, reads=[bps, CST], writes=[bRS])
            S.op("dve", "reciprocal", out=RS, in_=RS, reads=[bRS], writes=[bRS])
            S.op("dve", "scalar_tensor_tensor", out=X[:, tsl], in0=X[:, tsl], scalar=gcol, in1=RS, op0=ALU.mult, op1=ALU.mult,
                 reads=[bX, bg, bRS], writes=[bX])

    def attention(l):
        w_qkv = k.inp("w_qkv", [D, 3 * D])
        w_out = k.inp("w_out_na", [D, D])
        qkn_in = k.inp("qkn", [128, 2])
        etab = k.inp("etab", [8, 128, 2 * 14 * 64])
        mk_in = k.inp("mk", [128, 128])
        mctx_in = k.inp("mctx", [128, 1])
        ck_in = k.inp("ck_in", [16, 256, 64])
        cv_in = k.inp("cv_in", [16, 256, 64])
        ck_out = k.outp("ck", [4, 16, 256, 64])
        cv_out = k.outp("cv", [4, 16, 256, 64])
        qkn = k.sm(2)
        mk_flat = k.sm(128)
        mk = mk_flat.rearrange("p (i r) -> p i r", i=8)
        mctx = k.sm(1)
        bsm = k.buf("attn_small")
        S.dma(out=qkn, in_=qkn_in, writes=[bsm], queue="act")
        S.dma(out=mk_flat, in_=mk_in, writes=[bsm], queue="act")
        S.dma(out=mctx, in_=mctx_in, writes=[bsm], queue="act")
        QT = k.ar(GA, 1024); bQT = k.buf("QT")
        KT = k.ar(GA + 1024, 1024); bKT = k.buf("KT")
        VZf = k.ar(GA + 2048, 2048)
        VZ = VZf.rearrange("p (i h d) -> p i h d", i=8, h=2); bVZ = k.buf("VZ")
        KTOK = k.ar(GA + 4096, 1024).rearrange("p (i d) -> p i d", i=8); bKTOK = k.buf("KTOK")
        KCT = k.ar(GA + 5120, 256); bKCT = k.buf("KCT")
        VZCf = k.ar(GA + 5376, 512)
        VZC = VZCf.rearrange("p (c h d) -> p c h d", c=2, h=2); bVZC = k.buf("VZC")
        ETf = k.ar(GA + 5888, 1792)
        ET = ETf.rearrange("p (h i q) -> p h i q", h=2, i=14); bET = k.buf("ET")
        CKL = k.ar(GA + 7680, 256).rearrange("p (c d) -> p c d", c=2); bCKL = k.buf("CKL")
        CVL = k.ar(GA + 7936, 256).rearrange("p (c h d) -> p c h d", c=2, h=2); bCVL = k.buf("CVL")
        PB = [k.ar(GA + 8192 + i * 1024, 1024).rearrange("p (h n) -> p h n", h=2) for i in range(2)]; bPB = [k.buf(f"PB{i}") for i in range(2)]
        RDEN = k.ar(GA + 10240, 1024); bRDEN = k.buf("RDEN")
        assert GA + 11264 <= AW
        DZ = k.cst3[:, 0:256].rearrange("p (h d) -> p h d", h=2)
        DZC = k.sm(256).rearrange("p (h d) -> p h d", h=2)
        bDZC = k.buf("DZC")
        S.op("dve", "tensor_scalar", out=DZC, in0=DZ, scalar1=mctx[:, 0:1], scalar2=None, op0=ALU.mult, reads=[CST, bsm], writes=[bDZC])
        S.op("pool", "memset", ap=VZf, constant=0.0, writes=[bVZ])
        S.op("pool", "memset", ap=VZCf, constant=0.0, writes=[bVZC])
        NUM, DEN = k.psb2[2], k.psb2[3]
        bNUM = [k.PS[4], k.PS[5]]
        bDEN = [k.PS[6], k.PS[7]]
        p_rr = 0
        for hb in range(int(os.environ.get('ATT_NHB', 8))):
            i = k.wa_rr % 2
            k.wa_rr += 1
            wt, bwt = WA[i], bWA[i]
            for part in range(3):
                c0 = part * D + hb * 128
                S.dma(out=wt[:, :, part * 128:(part + 1) * 128], in_=w_qkv[:, c0:c0 + 128].rearrange("(k p) n -> p k n", p=128),
                      writes=[bwt.sub(part)])
            S.dma(out=ETf, in_=etab[hb], writes=[bET], queue="act")
            S.op("act", "activation", out=ETf, in_=ETf, func=AF.Exp, reads=[bET], writes=[bET])
            for hh in range(2):
                S.dma(out=CKL[:, :, hh * 64:(hh + 1) * 64], in_=ck_in[2 * hb + hh].rearrange("(c p) d -> p c d", p=128), writes=[bCKL], queue="act")
                S.dma(out=CVL[:, :, hh, :], in_=cv_in[2 * hb + hh].rearrange("(c p) d -> p c d", p=128), writes=[bCVL], queue="act")
            cast_rows(wt, bwt, 8, 384)
            for X, bX, part in ((QT, bQT, 0), (KT, bKT, 1)):
                for half in range(2):
                    tsl = slice(half * 512, (half + 1) * 512)
                    ps, bps = next_ps()
                    for kk in range(8):
                        mm(ps[:, :], wv(wt, kk, part * 128, (part + 1) * 128, 384), HTr(kk, half), kk == 0, kk == 7, [wsub(bwt, part), bHT], [bps])
                    evac(X[:, tsl], ps[:, :], reads=[bps], writes=[bX])
                head_norm(X, bX, qkn[:, part:part + 1], bsm)
            for tt in range(8):
                ps, bps = next_ps()
                for kk in range(8):
                    mm(ps[:, 0:128], HTt(kk, tt), wv(wt, kk, 256, 384, 384), kk == 0, kk == 7, [wsub(bwt, 2), bHT], [bps])
                dst = VZf[:, tt * 256:tt * 256 + 256].rearrange("p (h d) -> p h d", h=2)
                for hh in range(2):
                    evac(dst[:, hh, hh * 64:(hh + 1) * 64], ps[:, hh * 64:(hh + 1) * 64], reads=[bps], writes=[bVZ])
            for tq in range(2):
                ps, bps = next_ps()
                for j in range(4):
                    tt = tq * 4 + j
                    tr(ps[:, j * 128:(j + 1) * 128], KT[:, tt * 128:(tt + 1) * 128], [bKT], [bps])
                evac(KTOK[:, tq * 4:tq * 4 + 4, :], ps[:, :].rearrange("p (j d) -> p j d", j=4), reads=[bps], writes=[bKTOK])
            ps, bps = next_ps()
            for c in range(2):
                tr(ps[:, c * 128:(c + 1) * 128], CKL[:, c, :], [bCKL], [bps])
            evac(KCT, ps[:, 0:256], reads=[bps], writes=[bKCT])
            for hh in range(2):
                S.op("dve", "tensor_scalar", out=VZC[:, :, hh, hh * 64:(hh + 1) * 64], in0=CVL[:, :, hh, :], scalar1=mctx[:, 0:1], scalar2=None,
                     op0=ALU.mult, reads=[bCVL, bsm], writes=[bVZC])
            for hh in range(2):
                h = 2 * hb + hh
                for sg in range(4):
                    S.dma(out=ck_out[sg, h, :, :].rearrange("(c p) d -> p c d", p=128),
                          in_=KTOK[:, 2 * sg:2 * sg + 2, hh * 64:(hh + 1) * 64], reads=[bKTOK], queue="act")
                    S.dma(out=cv_out[sg, h, :, :].rearrange("(c p) d -> p c d", p=128),
                          in_=VZ[:, 2 * sg:2 * sg + 2, hh, hh * 64:(hh + 1) * 64], reads=[bVZ], queue="act")
            groups = []
            for ti in range(8):
                rows = [r for r in range(16) if (min(max(r - 4, 0), 8) // 2) <= ti <= ((min(max(r - 4, 0), 8) + 7) // 2)]
                for lo_, hi_ in ((0, 8), (8, 16)):
                    rr = [r for r in rows if lo_ <= r < hi_]
                    if rr:
                        assert rr == list(range(rr[0], rr[-1] + 1))
                        groups.append(("w", ti, rr[0], rr[-1] + 1))
            for c in range(2):
                groups.append(("c", c, 0, 8))
                groups.append(("c", c, 8, 16))
            touched = set()

            def g_scores(g):
                nonlocal p_rr
                kind, ci, r0, r1 = g
                n = (r1 - r0) * 64
                pp, bp0, bp1 = next_pair()
                for hh in range(2):
                    hs = slice(hh * 64, (hh + 1) * 64)
                    lhsT = (KT if kind == "w" else KCT)[hs, ci * 128:(ci + 1) * 128]
                    mm(pp[:, hh * 512:hh * 512 + n], lhsT, QT[hs, r0 * 64:r1 * 64], True, True, [bKT if kind == "w" else bKCT, bQT],
                       [bp0 if hh == 0 else bp1])
                pi = p_rr % 2
                p_rr += 1
                P = PB[pi]
                S.op("act", "activation", out=P[:, :, 0:n], in_=pp[:, :].rearrange("p (h q) -> p h q", h=2)[:, :, 0:n], func=AF.Exp, scale=0.125,
                     reads=[bp0, bp1], writes=[bPB[pi]])
                if kind == "w":
                    for r in range(r0, r1):
                        idx_e = r + 6 - 2 * ci
                        assert 0 <= idx_e <= 13, (r, ci, idx_e)
                        sl = slice((r - r0) * 64, (r - r0 + 1) * 64)
                        S.op("dve", "scalar_tensor_tensor", out=P[:, :, sl], in0=P[:, :, sl], scalar=mk[:, ci, r:r + 1], in1=ET[:, :, idx_e, :],
                             op0=ALU.mult, op1=ALU.mult, reads=[bPB[pi], bsm, bET], writes=[bPB[pi]])
                return pi

            def g_pv(g, pi):
                kind, ci, r0, r1 = g
                n = (r1 - r0) * 64
                bank = r0 // 8
                c0 = r0 * 64
                P = PB[pi]
                for hh in range(2):
                    for acc, bacc, which in ((NUM, bNUM, "n"), (DEN, bDEN, "d")):
                        if which == "n":
                            lhsT = VZ[:, ci, hh, :] if kind == "w" else VZC[:, ci, hh, :]
                            rd = [bVZ if kind == "w" else bVZC]
                        else:
                            lhsT = DZ[:, hh, :] if kind == "w" else DZC[:, hh, :]
                            rd = [CST, bDZC]
                        first = (which, bank) not in touched
                        touched.add((which, bank))
                        S.op("pe", "matmul", out=acc[:, c0:c0 + n], lhsT=lhsT, rhs=P[:, hh, 0:n], start=first, stop=False, skip_group_check=True,
                             reads=rd + [bPB[pi]], writes=[bacc[bank]])

            prev = None
            for g in groups:
                pi = g_scores(g)
                if prev is not None:
                    g_pv(*prev)
                prev = (g, pi)
            g_pv(*prev)
            for bank in range(2):
                tsl = slice(bank * 512, (bank + 1) * 512)
                S.op("dve", "reciprocal", out=RDEN[:, tsl], in_=DEN[:, tsl], reads=[bDEN[bank]], writes=[bRDEN])
                S.op("dve", "tensor_tensor", out=OA[:, hb, tsl], in0=NUM[:, tsl], in1=RDEN[:, tsl], op=ALU.mult,
                     reads=[bNUM[bank], bRDEN], writes=[bOA.sub(hb)])
        k.dump("oa", OA, [128, 8, 1024], [bOA])
        out_proj(w_out, l)

    ones_k = k.cst[:, 640:768]
    MU = k.cst[:, 768:896]
    ML = k.cst[:, 896:1024]
    GB = OFF_G

    def mixer_ab(l):
        w_in = k.inp("w_in", [D, DIN_AB])
        w_out = k.inp("w_out_ab", [D, D])
        lb_in = k.inp("lb_in", [128, 12])
        gn_in = k.inp("gn_in", [128, 2])
        keep_in = k.inp("keep", [128, 1])
        sth_init = k.inp("sth_init", [4, 2, 4, 128, 128])
        sth_out = k.outp("st_h", [4, 2, 4, 128, 128])
        bms = k.buf("mix_small")
        lbe = k.sm(12)
        gn = k.sm(2)
        keep = k.sm(1)
        S.dma(out=lbe, in_=lb_in, writes=[bms], queue="act")
        S.dma(out=gn, in_=gn_in, writes=[bms], queue="act")
        S.dma(out=keep, in_=keep_in, writes=[bms], queue="act")
        lbs = k.sm(4)
        oml = k.sm(4)
        lsum = k.sm(4)
        blb = k.buf("lb")
        S.op("act", "activation", out=lbe, in_=lbe, func=AF.Exp, reads=[bms], writes=[bms])
        S.op("dve", "tensor_tensor", out=lsum, in0=lbe[:, 0:4], in1=lbe[:, 4:8], op=ALU.add, reads=[bms], writes=[blb])
        S.op("dve", "tensor_tensor", out=lsum, in0=lsum, in1=lbe[:, 8:12], op=ALU.add, reads=[bms, blb], writes=[blb])
        S.op("dve", "reciprocal", out=lsum, in_=lsum, reads=[blb], writes=[blb])
        S.op("dve", "tensor_tensor", out=lbs, in0=lbe[:, 0:4], in1=lsum, op=ALU.mult, reads=[bms, blb], writes=[blb])
        S.op("dve", "tensor_scalar", out=oml, in0=lbs, scalar1=-1.0, scalar2=1.0, op0=ALU.mult, op1=ALU.add, reads=[blb], writes=[blb])

        def big(i):
            return k.ar(GB + i * 1024, 1024)

        QS, bQS = big(0), k.buf("QS")
        GAT, bGAT = big(1), k.buf("GAT")
        VTOK, bVTOK = big(2).rearrange("p (c d) -> p c d", c=8), k.buf("VTOK")
        A_, bA = big(3), k.buf("hA")
        B_, bB = big(4), k.buf("hB")
        C_, bC = big(5), k.buf("hC")
        D_, bD = big(6), k.buf("hD")
        E_, bE = big(7), k.buf("hE")
        F_, bF = big(8), k.buf("hF")
        KZ, bKZ = k.ar(GB + 12288, 1024), k.buf("hKZ")
        base2 = GB + 9 * 1024
        KD = [k.ar(base2 + i * 128, 128) for i in range(2)]; bKD = [k.buf(f"KD{i}") for i in range(2)]
        AT = [k.ar(base2 + 256 + i * 128, 128) for i in range(2)]; bAT = [k.buf(f"AT{i}") for i in range(2)]
        SS = [k.ar(base2 + 512 + i * 128, 128) for i in range(2)]; bSS = [k.buf(f"SS{i}") for i in range(2)]
        SI = [k.ar(base2 + 768 + i * 128, 128) for i in range(2)]; bSI = [k.buf(f"SI{i}") for i in range(2)]
        TOT = k.sm(8); GM = k.sm(8); GL = k.sm(8); EGL = k.sm(8)
        bTOT, bGM, bGL, bEGL = k.buf("TOT"), k.buf("GM"), k.buf("GL"), k.buf("EGL")
        C3 = C_.rearrange("p (c t) -> p c t", c=8)
        B3 = B_.rearrange("p (c t) -> p c t", c=8)

        def proj_T(wt, bwt, blk, fn):
            for half in range(2):
                tsl = slice(half * 512, (half + 1) * 512)
                ps, bps = next_ps()
                for kk in range(8):
                    mm(ps[:, :], wv(wt, kk, blk * 128, (blk + 1) * 128, wt_cols[id(bwt)]), HTr(kk, half), kk == 0, kk == 7, [wsub(bwt, blk), bHT], [bps])
                fn(tsl, ps, bps)

        wt_cols = {}

        def load_blocks(cols):
            i = k.wa_rr % 2
            k.wa_rr += 1
            for j, c0 in enumerate(cols):
                S.dma(out=WA[i][:, :, j * 128:(j + 1) * 128], in_=w_in[:, c0:c0 + 128].rearrange("(k p) n -> p k n", p=128),
                      writes=[bWA[i].sub(j)])
            wt_cols[id(bWA[i])] = 128 * len(cols)
            cast_rows(WA[i], bWA[i], 8, 128 * len(cols))
            return WA[i], bWA[i]

        kid = [0]

        def hgrn(hd):
            wt, bwt = load_blocks([0 + 128 * hd, 2048 + 128 * hd, 1536 + 128 * hd, 512 + 128 * hd])
            wt2, bwt2 = load_blocks([1024 + 128 * hd])
            proj_T(wt, bwt, 0, lambda tsl, ps, bps: S.op("act", "activation", out=QS[:, tsl], in_=ps[:, :], func=AF.Silu, reads=[bps], writes=[bQS]))
            S.op("pool", "tensor_scalar", out=QS, in0=QS, scalar1=float(128 ** -0.5), scalar2=None, op0=ALU.mult, reads=[bQS], writes=[bQS])
            proj_T(wt, bwt, 1, lambda tsl, ps, bps: S.op("act", "activation", out=GAT[:, tsl], in_=ps[:, :], func=AF.Silu, reads=[bps], writes=[bGAT]))
            for tt in range(8):
                ps, bps = next_ps()
                for kk in range(8):
                    mm(ps[:, 0:128], HTt(kk, tt), wv(wt, kk, 256, 384, 512), kk == 0, kk == 7, [wsub(bwt, 2), bHT], [bps])
                evac(VTOK[:, tt, :], ps[:, 0:128], reads=[bps], writes=[bVTOK])
            for d in range(2):
                wf, bwf, blk = (wt, bwt, 3) if d == 0 else (wt2, bwt2, 0)
                proj_T(wf, bwf, blk, lambda tsl, ps, bps: S.op("act", "activation", out=A_[:, tsl], in_=ps[:, :], func=AF.Sigmoid, reads=[bps], writes=[bA]))
                S.op("dve", "tensor_scalar", out=A_, in0=A_, scalar1=oml[:, hd:hd + 1], scalar2=lbs[:, hd:hd + 1], op0=ALU.mult, op1=ALU.add,
                     reads=[bA, blb], writes=[bA])
                S.op("act", "activation", out=B_, in_=A_, func=AF.Ln, reads=[bA], writes=[bB])
                for c in range(8):
                    S.op("dve", "tensor_tensor_scan", out=C3[:, c, :], data0=ones1, data1=B3[:, c, :], initial=0.0, op0=ALU.mult, op1=ALU.add,
                         reads=[bB, CST], writes=[bC])
                S.op("pool", "tensor_scalar", out=A_, in0=A_, scalar1=-1.0, scalar2=1.0, op0=ALU.mult, op1=ALU.add, reads=[bA], writes=[bA])
                if d == 1:
                    S.op("pool", "tensor_copy", out=TOT, in_=C3[:, :, 127], reads=[bC], writes=[bTOT])
                    S.op("dve", "tensor_tensor", out=C3, in0=TOT.unsqueeze(2).to_broadcast([128, 8, 128]), in1=C3, op=ALU.subtract,
                         reads=[bTOT, bC], writes=[bC])
                    S.op("dve", "tensor_tensor", out=C_, in0=C_, in1=B_, op=ALU.add, reads=[bC, bB], writes=[bC])
                mid, last = (63, 127) if d == 0 else (64, 0)
                S.op("pool", "tensor_copy", out=GM, in_=C3[:, :, mid], reads=[bC], writes=[bGM])
                S.op("pool", "tensor_copy", out=GL, in_=C3[:, :, last], reads=[bC], writes=[bGL])
                S.op("act", "activation", out=EGL, in_=GL, func=AF.Exp, reads=[bGL], writes=[bEGL])
                S.op("dve", "tensor_tensor", out=B3, in0=C3, in1=GM.unsqueeze(2).to_broadcast([128, 8, 128]), op=ALU.subtract,
                     reads=[bC, bGM], writes=[bB])
                S.op("act", "activation", out=D_, in_=B_, func=AF.Exp, reads=[bB], writes=[bD])
                S.op("dve", "tensor_tensor", out=D_, in0=D_, in1=QS, op=ALU.mult, reads=[bD, bQS], writes=[bD])
                S.op("act", "activation", out=E_, in_=B_, func=AF.Exp, scale=-1.0, reads=[bB], writes=[bE])
                S.op("pool", "tensor_tensor", out=E_, in0=E_, in1=A_, op=ALU.mult, reads=[bE, bA], writes=[bE])
                kh = 0 if d == 0 else 1
                S.op("pool", "memset", ap=KZ, constant=0.0, writes=[bKZ])
                S.op("pool", "tensor_copy", out=KZ.rearrange("p (c h t) -> p c h t", c=8, h=2)[:, :, kh, :],
                     in_=E_.rearrange("p (c h t) -> p c h t", c=8, h=2)[:, :, kh, :], reads=[bE], writes=[bKZ])
                S.op("dve", "tensor_tensor", out=B3, in0=GL.unsqueeze(2).to_broadcast([128, 8, 128]), in1=C3, op=ALU.subtract,
                     reads=[bC, bGL], writes=[bB])
                S.op("act", "activation", out=F_, in_=B_, func=AF.Exp, reads=[bB], writes=[bF])
                S.op("pool", "tensor_tensor", out=F_, in0=F_, in1=A_, op=ALU.mult, reads=[bF, bA], writes=[bF])
                S.op("act", "activation", out=C_, in_=C_, func=AF.Exp, reads=[bC], writes=[bC])
                S.op("dve", "tensor_tensor", out=C_, in0=C_, in1=QS, op=ALU.mult, reads=[bC, bQS], writes=[bC])
                msk = MU if d == 0 else ML
                order = list(range(8)) if d == 0 else list(range(7, -1, -1))
                stc = {"cur": None}

                def h_pre(n, c):
                    csl = slice(c * 128, (c + 1) * 128)
                    i2 = n % 2
                    ps, bps = next_ps()
                    h1 = slice(c * 128, c * 128 + 64)
                    h2 = slice(c * 128 + 64, (c + 1) * 128)
                    mm(ps[:, 0:64], (KZ if d == 0 else E_)[:, csl], D_[:, h1], True, True, [bE, bKZ, bD], [bps])
                    mm(ps[:, 64:128], (E_ if d == 0 else KZ)[:, csl], D_[:, h2], True, True, [bE, bKZ, bD], [bps])
                    S.op("dve", "tensor_tensor", out=AT[i2], in0=ps[:, 0:128], in1=msk, op=ALU.mult, reads=[bps, CST], writes=[bAT[i2]])
                    ps2, bps2 = next_ps()
                    tr(ps2[:, 0:128], F_[:, csl], [bF], [bps2])
                    evac(KD[i2], ps2[:, 0:128], reads=[bps2], writes=[bKD[i2]])

                def h_main(n, c):
                    csl = slice(c * 128, (c + 1) * 128)
                    seg = c // 2
                    i2 = n % 2
                    cur = stc["cur"]
                    if n % 2 == 0:
                        S.dma(out=SI[i2], in_=sth_init[seg, d, hd], writes=[bSI[i2]], queue="act")
                        nxt = (0 if cur is None else 1 - cur)
                        if cur is None:
                            S.op("pool", "tensor_copy", out=SS[nxt], in_=SI[i2], reads=[bSI[i2]], writes=[bSS[nxt]])
                        else:
                            S.op("dve", "scalar_tensor_tensor", out=SS[nxt], in0=SS[cur], scalar=keep[:, 0:1], in1=SI[i2], op0=ALU.mult, op1=ALU.add,
                                 reads=[bSS[cur], bms, bSI[i2]], writes=[bSS[nxt]])
                        cur = nxt
                    pa, (bpa, _) = next_acc()
                    mm(pa[:, 0:128], VTOK[:, c, :], AT[i2], True, False, [bVTOK, bAT[i2]], [bpa])
                    mm(pa[:, 0:128], SS[cur], C_[:, csl], False, True, [bSS[cur], bC], [bpa])
                    if d == 0:
                        evac(OA[:, hd, csl], pa[:, 0:128], reads=[bpa], writes=[bOA.sub(hd)])
                    else:
                        S.op("dve", "tensor_tensor", out=OA[:, hd, csl], in0=pa[:, 0:128], in1=OA[:, hd, csl], op=ALU.add,
                             reads=[bpa, bOA.sub(hd)], writes=[bOA.sub(hd)])
                    pb, (bpb, _) = next_acc()
                    mm(pb[:, 0:128], KD[i2], VTOK[:, c, :], True, True, [bKD[i2], bVTOK], [bpb])
                    nxt = 1 - cur
                    S.op("dve", "scalar_tensor_tensor", out=SS[nxt], in0=SS[cur], scalar=EGL[:, c:c + 1], in1=pb[:, 0:128], op0=ALU.mult, op1=ALU.add,
                         reads=[bSS[cur], bEGL, bpb], writes=[bSS[nxt]])
                    cur = nxt
                    if n % 2 == 1:
                        S.dma(out=sth_out[seg, d, hd], in_=SS[cur], reads=[bSS[cur]], queue="act")
                    stc["cur"] = cur

                h_pre(0, order[0])
                for n, c in enumerate(order):
                    if n + 1 < 8:
                        h_pre(n + 1, order[n + 1])
                    h_main(n, c)
            for half in range(2):
                tsl = slice(half * 512, (half + 1) * 512)
                i = sq_rr[0] % 3
                sq_rr[0] += 1
                S.op("act", "activation", out=SQ2[i], in_=OA[:, hd, tsl], func=AF.Square, reads=[bOA.sub(hd)], writes=[bSQ[i]])
                ps, bps = next_ps()
                mm(ps[:, :], ones_k, SQ2[i], True, True, [bSQ[i], CST], [bps])
                S.op("act", "activation", out=RS2, in_=ps[:, :], func=AF.Sqrt, bias=epsc, scale=1.0, reads=[bps, CST], writes=[bRS])
                S.op("dve", "reciprocal", out=RS2, in_=RS2, reads=[bRS], writes=[bRS])
                S.op("dve", "scalar_tensor_tensor", out=OA[:, hd, tsl], in0=OA[:, hd, tsl], scalar=gn[:, 0:1], in1=RS2, op0=ALU.mult, op1=ALU.mult,
                     reads=[bOA.sub(hd), bms, bRS], writes=[bOA.sub(hd)])
                S.op("pool", "tensor_tensor", out=OA[:, hd, tsl], in0=OA[:, hd, tsl], in1=GAT[:, tsl], op=ALU.mult,
                     reads=[bOA.sub(hd), bGAT], writes=[bOA.sub(hd)])

        SQ2 = [k.ar(base2 + 1024 + i * 512, 512) for i in range(3)]
        RS2 = k.ar(base2 + 1024 + 3 * 512, 512)
        for hd in range(int(os.environ.get("MIX_NH", 4))):
            hgrn(hd)
        S.barrier()

        conv_in = k.inp("conv_in", [128, 60])
        albt_in = k.inp("albt_in", [128, 16])
        stg_init = k.inp("stg_init", [4, 2, 4, 128, 128])
        stg_out = k.outp("st_g", [4, 2, 4, 128, 128])
        cw = k.sm(60)
        albt = k.sm(16)
        S.dma(out=cw, in_=conv_in, writes=[bms], queue="act")
        S.dma(out=albt, in_=albt_in, writes=[bms], queue="act")
        nea = k.sm(8)
        bnea = k.buf("nea")
        S.op("act", "activation", out=nea, in_=albt[:, 0:8], func=AF.Exp, reads=[bms], writes=[bnea])
        S.op("pool", "tensor_scalar", out=nea, in0=nea, scalar1=-1.0, scalar2=None, op0=ALU.mult, reads=[bnea], writes=[bnea])
        GLOG = k.sm(64).rearrange("p (c j) -> p c j", c=8)
        BETA = k.sm(64).rearrange("p (c j) -> p c j", c=8)
        GCOL = k.sm(64).rearrange("p (c j) -> p c j", c=8)
        EGCOL = k.sm(64).rearrange("p (c j) -> p c j", c=8)
        bGLOG, bBETA, bGCOL, bEGCOL = k.buf("GLOG"), k.buf("BETA"), k.buf("GCOL"), k.buf("EGCOL")
        i = k.wa_rr % 2
        k.wa_rr += 1
        wsm, bwsm = WA[i], bWA[i]
        S.dma(out=wsm[:, :, 0:16], in_=w_in[:, 4608:4624].rearrange("(k p) n -> p k n", p=128), writes=[bwsm])
        cast_rows(wsm, bwsm, 8, 16)
        for tt in range(8):
            ps, bps = next_ps()
            for kk in range(8):
                mm(ps[:, 0:16], HTt(kk, tt), wv(wsm, kk, 0, 16, 16), kk == 0, kk == 7, [bwsm, bHT], [bps])
            S.op("dve", "tensor_tensor", out=GLOG[:, tt, :], in0=ps[:, 0:8], in1=albt[:, 8:16], op=ALU.add, reads=[bps, bms], writes=[bGLOG])
            S.op("act", "activation", out=BETA[:, tt, :], in_=ps[:, 8:16], func=AF.Sigmoid, reads=[bps, bGLOG], writes=[bBETA])
        GLOGf = GLOG.rearrange("p c j -> p (c j)")
        S.op("act", "activation", out=GLOGf, in_=GLOGf, func=AF.Exp, reads=[bGLOG], writes=[bGLOG])
        S.op("act", "activation", out=GLOGf, in_=GLOGf, func=AF.Ln, bias=ones1[:, 0:1], scale=1.0, reads=[bGLOG, CST], writes=[bGLOG])
        S.op("dve", "tensor_tensor", out=GLOG, in0=GLOG, in1=nea.unsqueeze(1).to_broadcast([128, 8, 8]), op=ALU.mult,
             reads=[bGLOG, bnea], writes=[bGLOG])
        for tt in range(8):
            ps, bps = next_ps()
            mm(ps[:, 0:4], MU, GLOG[:, tt, 0:4], True, True, [bGLOG, CST], [bps])
            mm(ps[:, 4:8], ML, GLOG[:, tt, 4:8], True, True, [bGLOG, CST], [bps])
            evac(GCOL[:, tt, :], ps[:, 0:8], reads=[bps], writes=[bGCOL])
        S.op("act", "activation", out=EGCOL.rearrange("p c j -> p (c j)"), in_=GCOL.rearrange("p c j -> p (c j)"), func=AF.Exp,
             reads=[bGCOL], writes=[bEGCOL])

        QT_, bQT_ = big(0), k.buf("gQT")
        KT_, bKT_ = big(1), k.buf("gKT")
        GBT, bGBT = big(2), k.buf("GBT")
        KTK, bKTK = big(3).rearrange("p (c d) -> p c d", c=8), k.buf("gKTK")
        VTK, bVTK = big(4).rearrange("p (c d) -> p c d", c=8), k.buf("gVTK")
        VT_, bVT_ = big(5), k.buf("gVT")
        PAD = k.ar(GB + 6 * 1024, 1040).rearrange("p (s t) -> p s t", s=4)
        bPAD = k.buf("PAD")
        SQ3 = [k.ar(GB + 5 * 1024 + i * 512, 512) for i in range(3)]
        RS3 = k.ar(GB + 5 * 1024 + 3 * 512, 512)
        TB = GB + 5 * 1024 + 2064

        TSZ = 3840

        def tmpset(d):
            o = [TB + d * TSZ]

            def t(n, name):
                a = k.ar(o[0], n)
                o[0] += n
                return a, k.buf(f"g{d}_{name}")
            T_ = {}
            for name, n in (("TG", 128), ("XY0", 256), ("XY1", 256), ("RR0", 128), ("RR1", 128), ("LTr", 128), ("LTs", 128), ("LTm", 128),
                            ("kb", 128), ("kbT", 128), ("bV", 128), ("kbg", 128), ("VN", 128), ("S0", 128), ("S1", 128), ("SI", 128)):
                T_[name] = t(n, name)
            for name, n in (("UW", 256), ("attnT", 128), ("kdec", 128), ("EG", 128), ("qgT", 128)):
                T_[name] = [t(n, name + "a"), t(n, name + "b")]
            assert o[0] - (TB + d * TSZ) == TSZ, o[0] - (TB + d * TSZ)
            return T_
        TM = [tmpset(0), tmpset(1)]
        assert TB + 2 * TSZ <= AW, (TB + 2 * TSZ, AW)
        NSU = k.cst2[:, 0:128]
        NSL = k.cst2[:, 128:256]

        def l2n(X, bX, scale):
            for half in range(2):
                tsl = slice(half * 512, (half + 1) * 512)
                i = sq_rr[0] % 3
                sq_rr[0] += 1
                S.op("act", "activation", out=SQ3[i], in_=X[:, tsl], func=AF.Square, reads=[bX], writes=[bSQ[i]])
                ps, bps = next_ps()
                mm(ps[:, :], ones1, SQ3[i], True, True, [bSQ[i], CST], [bps])
                S.op("act", "activation", out=RS3, in_=ps[:, :], func=AF.Sqrt, bias=epsc, scale=1.0, reads=[bps, CST], writes=[bRS])
                S.op("dve", "reciprocal", out=RS3, in_=RS3, reads=[bRS], writes=[bRS])
                S.op("dve", "scalar_tensor_tensor", out=X[:, tsl], in0=X[:, tsl], scalar=float(scale), in1=RS3, op0=ALU.mult, op1=ALU.mult,
                     reads=[bX, bRS], writes=[bX])

        def gdn(hd):
            wt, bwt = load_blocks([2560 + 128 * hd, 3072 + 128 * hd, 3584 + 128 * hd, 4096 + 128 * hd])
            S.op("pool", "memset", ap=PAD, constant=0.0, writes=[bPAD])
            S.op("pool", "memset", ap=OA[:, 4 + hd, :], constant=0.0, writes=[bOA.sub(4 + hd)])
            for blk, (Y, bY) in enumerate(((QT_, bQT_), (KT_, bKT_), (VT_, bVT_))):
                proj_T(wt, bwt, blk, lambda tsl, ps, bps: evac(PAD[:, tsl.start // 256:tsl.start // 256 + 2, 2:258],
                                                                  ps[:, :].rearrange("p (s t) -> p s t", s=2), reads=[bps], writes=[bPAD]))
                S.op("pool", "tensor_scalar", out=PAD[:, 1:4, 0:2], in0=PAD[:, 0:3, 256:258], scalar1=keep[:, 0:1], scalar2=None, op0=ALU.mult,
                     reads=[bPAD, bms], writes=[bPAD])
                S.op("pool", "tensor_scalar", out=PAD[:, 0:3, 258:260], in0=PAD[:, 1:4, 2:4], scalar1=keep[:, 0:1], scalar2=None, op0=ALU.mult,
                     reads=[bPAD, bms], writes=[bPAD])
                Y3 = Y.rearrange("p (s t) -> p s t", s=4)
                cb = (blk * 4 + hd) * 5
                S.op("dve", "tensor_scalar", out=Y3, in0=PAD[:, :, 0:256], scalar1=cw[:, cb:cb + 1], scalar2=None, op0=ALU.mult,
                     reads=[bPAD, bms], writes=[bY])
                for j in range(1, 5):
                    S.op("dve", "scalar_tensor_tensor", out=Y3, in0=PAD[:, :, j:j + 256], scalar=cw[:, cb + j:cb + j + 1], in1=Y3,
                         op0=ALU.mult, op1=ALU.add, reads=[bPAD, bms, bY], writes=[bY])
                S.op("act", "activation", out=Y, in_=Y, func=AF.Silu, reads=[bY], writes=[bY])
            proj_T(wt, bwt, 3, lambda tsl, ps, bps: S.op("act", "activation", out=GBT[:, tsl], in_=ps[:, :], func=AF.Silu, reads=[bps], writes=[bGBT]))
            for X, bX, XK, bXK in ((VT_, bVT_, VTK, bVTK),):
                for tq in range(2):
                    ps, bps = next_ps()
                    for j in range(4):
                        tt = tq * 4 + j
                        tr(ps[:, j * 128:(j + 1) * 128], X[:, tt * 128:(tt + 1) * 128], [bX], [bps])
                    evac(XK[:, tq * 4:tq * 4 + 4, :], ps[:, :].rearrange("p (j d) -> p j d", j=4), reads=[bps], writes=[bXK])
            S.barrier()
            l2n(QT_, bQT_, 128 ** -0.5)
            l2n(KT_, bKT_, 1.0)
            for tq in range(2):
                ps, bps = next_ps()
                for j in range(4):
                    tt = tq * 4 + j
                    tr(ps[:, j * 128:(j + 1) * 128], KT_[:, tt * 128:(tt + 1) * 128], [bKT_], [bps])
                evac(KTK[:, tq * 4:tq * 4 + 4, :], ps[:, :].rearrange("p (j d) -> p j d", j=4), reads=[bps], writes=[bKTK])

            def chunk_gen(d, n, c, st, phase):
                T_ = TM[d]
                col = d * 4 + hd
                gl = GLOG[:, c, col:col + 1]
                be = BETA[:, c, col:col + 1]
                gcol = GCOL[:, c, col:col + 1]
                egcol = EGCOL[:, c, col:col + 1]
                TRI, MINC, NSTR, last = (MU, MU, NSU, 127) if d == 0 else (ML, ML, NSL, 0)
                csl = slice(c * 128, (c + 1) * 128)
                seg = c // 2
                par = n % 2
                (TG, bTG), (LTr, bLTr), (LTs, bLTs), (LTm, bLTm) = T_["TG"], T_["LTr"], T_["LTs"], T_["LTm"]
                (kb, bkb), (kbT, bkbT), (bV, bbV), (kbg, bkbg), (VN, bVN), (SI, bSI_) = T_["kb"], T_["kbT"], T_["bV"], T_["kbg"], T_["VN"], T_["SI"]
                (UW, bUW), (attnT, battnT), (kdec, bkdec), (EG, bEG), (qgT, bqgT) = (T_["UW"][par], T_["attnT"][par], T_["kdec"][par],
                                                                                       T_["EG"][par], T_["qgT"][par])
                XY = [T_["XY0"], T_["XY1"]]
                RR = [T_["RR0"], T_["RR1"]]
                SSg = [T_["S0"], T_["S1"]]
                if phase == "pre":
                    S.op("act", "activation", out=TG, in_=TRI, func=AF.Copy, scale=gl, reads=[CST, bGLOG], writes=[bTG])
                    yield
                    psg, bpsg = next_ps()
                    mm(psg[:, 0:128], ones1, TG, True, True, [CST, bTG], [bpsg])
                    S.op("dve", "tensor_scalar", out=LTr, in0=psg[:, 0:128], scalar1=gcol, scalar2=k.cst2[:, 0:1], op0=ALU.subtract, op1=ALU.min,
                         reads=[bpsg, bGCOL], writes=[bLTr])
                    S.op("act", "activation", out=EG, in_=psg[:, 0:128], func=AF.Exp, reads=[bpsg, bLTr], writes=[bEG])
                    S.op("act", "activation", out=LTr, in_=LTr, func=AF.Exp, reads=[bLTr], writes=[bLTr])
                    S.op("dve", "tensor_tensor", out=LTs, in0=LTr, in1=NSTR, op=ALU.mult, reads=[bLTr, CST], writes=[bLTs])
                    S.op("pool", "tensor_tensor", out=LTm, in0=LTr, in1=MINC, op=ALU.mult, reads=[bLTr, CST], writes=[bLTm])
                    yield
                    S.op("act", "activation", out=kb, in_=KTK[:, c, :], func=AF.Copy, scale=be, reads=[bKTK, bBETA], writes=[bkb])
                    yield
                    ps, bps = next_ps()
                    tr(ps[:, 0:128], kb, [bkb], [bps])
                    evac(kbT, ps[:, 0:128], reads=[bps], writes=[bkbT])
                    yield
                    ps, bps = next_ps()
                    mm(ps[:, 0:128], KT_[:, csl], kbT, True, True, [bKT_, bkbT], [bps])
                    mm(ps[:, 128:256], KT_[:, csl], QT_[:, csl], True, True, [bKT_, bQT_], [bps])
                    (XYa, bXYa) = XY[0]
                    S.op("dve", "tensor_tensor", out=XYa[:, 0:128], in0=ps[:, 0:128], in1=LTs, op=ALU.mult, reads=[bps, bLTs], writes=[bXYa])
                    S.op("dve", "tensor_tensor", out=attnT, in0=ps[:, 128:256], in1=LTm, op=ALU.mult, reads=[bps, bLTm], writes=[battnT])
                    yield
                    ps, bps = next_ps()
                    tr(ps[:, 0:128], XYa[:, 0:128], [bXYa], [bps])
                    evac(XYa[:, 128:256], ps[:, 0:128], reads=[bps], writes=[bXYa])
                    (RRa, bRRa) = RR[0]
                    S.op("dve", "tensor_tensor", out=RRa[:, 0:128], in0=XYa[:, 0:128], in1=ident, op=ALU.add, reads=[bXYa, CST], writes=[bRRa])
                    yield
                    cur = 0
                    for lev in range(6):
                        (XYc, bXYc), (XYn, bXYn) = XY[cur], XY[1 - cur]
                        (RRc, bRRc), (RRn, bRRn) = RR[cur], RR[1 - cur]
                        ps, bps = next_ps()
                        mm(ps[:, 128:256], XYc[:, 0:128], XYc[:, 128:256], True, True, [bXYc], [bps])
                        if lev < 5:
                            mm(ps[:, 0:128], XYc[:, 128:256], XYc[:, 0:128], True, True, [bXYc], [bps])
                            evac(XYn, ps[:, 0:256], reads=[bps], writes=[bXYn])
                        else:
                            evac(XYn[:, 128:256], ps[:, 128:256], reads=[bps], writes=[bXYn])
                        yield
                        ps2, bps2 = next_ps()
                        mm(ps2[:, 0:128], XYn[:, 128:256], RRc[:, 0:128], True, True, [bRRc, bXYn], [bps2])
                        S.op("dve", "tensor_tensor", out=RRn[:, 0:128], in0=ps2[:, 0:128], in1=RRc[:, 0:128], op=ALU.add, reads=[bps2, bRRc], writes=[bRRn])
                        cur = 1 - cur
                        yield
                    (Rf, bRf) = RR[cur]
                    R = Rf[:, 0:128]
                    S.op("act", "activation", out=bV, in_=VTK[:, c, :], func=AF.Copy, scale=be, reads=[bVTK, bBETA], writes=[bbV])
                    S.op("act", "activation", out=kbg, in_=kb, func=AF.Copy, scale=egcol, reads=[bkb, bEGCOL], writes=[bkbg])
                    yield
                    ps, bps = next_ps()
                    mm(ps[:, 0:128], R, bV, True, True, [bRf, bbV], [bps])
                    mm(ps[:, 128:256], kbg, R, True, True, [bRf, bkbg], [bps])
                    evac(UW, ps[:, 0:256], reads=[bps], writes=[bUW])
                    S.op("act", "activation", out=kdec, in_=KTK[:, c, :], func=AF.Copy, scale=LTr[:, last:last + 1],
                         reads=[bKTK, bLTr], writes=[bkdec])
                    S.op("dve", "tensor_tensor", out=qgT, in0=QT_[:, csl], in1=EG, op=ALU.mult, reads=[bQT_, bEG], writes=[bqgT])
                    yield
                    return
                if n % 2 == 0:
                    S.dma(out=SI, in_=stg_init[seg, d, hd], writes=[bSI_], queue="act")
                    nxt = 0 if st["cur"] is None else 1 - st["cur"]
                    if st["cur"] is None:
                        S.op("pool", "tensor_copy", out=SSg[nxt][0], in_=SI, reads=[bSI_], writes=[SSg[nxt][1]])
                    else:
                        S.op("dve", "scalar_tensor_tensor", out=SSg[nxt][0], in0=SSg[st["cur"]][0], scalar=keep[:, 0:1], in1=SI, op0=ALU.mult, op1=ALU.add,
                             reads=[SSg[st["cur"]][1], bms, bSI_], writes=[SSg[nxt][1]])
                    st["cur"] = nxt
                Sc, bSc = SSg[st["cur"]]
                ps, bps = next_ps()
                mm(ps[:, 0:128], UW[:, 128:256], Sc, True, True, [bUW, bSc], [bps])
                S.op("dve", "tensor_tensor", out=VN, in0=UW[:, 0:128], in1=ps[:, 0:128], op=ALU.subtract, reads=[bUW, bps], writes=[bVN])
                yield
                pa, (bpa, _) = next_acc()
                mm(pa[:, 0:128], Sc, qgT, True, False, [bSc, bqgT], [bpa])
                mm(pa[:, 0:128], VN, attnT, False, True, [bVN, battnT], [bpa])
                ob = bOA.sub(4 + hd)
                S.op("dve", "tensor_tensor", out=OA[:, 4 + hd, csl], in0=pa[:, 0:128], in1=OA[:, 4 + hd, csl], op=ALU.add, reads=[bpa, ob], writes=[ob])
                pb, (bpb, _) = next_acc()
                mm(pb[:, 0:128], kdec, VN, True, True, [bkdec, bVN], [bpb])
                Sn, bSn = SSg[1 - st["cur"]]
                S.op("dve", "scalar_tensor_tensor", out=Sn, in0=Sc, scalar=EG[:, last:last + 1], in1=pb[:, 0:128], op0=ALU.mult, op1=ALU.add,
                     reads=[bSc, bEG, bpb], writes=[bSn])
                st["cur"] = 1 - st["cur"]
                if n % 2 == 1 and not os.environ.get("GDN_NOOUT"):
                    S.dma(out=stg_out[seg, d, hd], in_=Sn, reads=[bSn], queue="act")
                yield

            sts = [{"cur": None}, {"cur": None}]

            def run_rr(gens):
                alive = list(gens)
                while alive:
                    for g in list(alive):
                        try:
                            next(g)
                        except StopIteration:
                            alive.remove(g)

            run_rr([chunk_gen(0, 0, 0, sts[0], "pre"), chunk_gen(1, 0, 7, sts[1], "pre")])
            for n in range(8):
                gens = [chunk_gen(0, n, n, sts[0], "scan"), chunk_gen(1, n, 7 - n, sts[1], "scan")]
                if n + 1 < 8:
                    gens += [chunk_gen(0, n + 1, n + 1, sts[0], "pre"), chunk_gen(1, n + 1, 7 - (n + 1), sts[1], "pre")]
                run_rr(gens)
            for half in range(2):
                tsl = slice(half * 512, (half + 1) * 512)
                i = sq_rr[0] % 3
                sq_rr[0] += 1
                ob = bOA.sub(4 + hd)
                S.op("act", "activation", out=SQ3[i], in_=OA[:, 4 + hd, tsl], func=AF.Square, reads=[ob], writes=[bSQ[i]])
                ps, bps = next_ps()
                mm(ps[:, :], ones_k, SQ3[i], True, True, [bSQ[i], CST], [bps])
                S.op("act", "activation", out=RS3, in_=ps[:, :], func=AF.Sqrt, bias=epsc, scale=1.0, reads=[bps, CST], writes=[bRS])
                S.op("dve", "reciprocal", out=RS3, in_=RS3, reads=[bRS], writes=[bRS])
                S.op("dve", "scalar_tensor_tensor", out=OA[:, 4 + hd, tsl], in0=OA[:, 4 + hd, tsl], scalar=gn[:, 1:2], in1=RS3, op0=ALU.mult, op1=ALU.mult,
                     reads=[ob, bms, bRS], writes=[ob])
                S.op("pool", "tensor_tensor", out=OA[:, 4 + hd, tsl], in0=OA[:, 4 + hd, tsl], in1=GBT[:, tsl], op=ALU.mult,
                     reads=[ob, bGBT], writes=[ob])
            S.barrier()

        for hd in range(int(os.environ.get("MIX_NG", 4))):
            gdn(hd)
        S.barrier()
        k.dump("oa", OA, [128, 8, 1024], [bOA])
        out_proj(w_out, l)

    k.dump("xt", XT, [128, 8, 1024], [bXT])
    k.dump("gsc", gsc, [128, 32], [bgsc])
    if "all" in stages:
        norm_mod(0, 0)
        S.barrier()
        mixer_ab(0)
        S.barrier()
        norm_mod(0, 1)
        S.barrier()
        mlp(0)
        S.barrier()
        norm_mod(1, 0)
        S.barrier()
        attention(1)
        S.barrier()
        norm_mod(1, 1)
        S.barrier()
        mlp(1)
    else:
        if "norm" in stages:
            norm_mod(0, 1)
            k.dump("rs", RS, [128, 512], [bRS])
            k.dump("ht", HT, [128, 8, 1024], [bHT])
        if "mlp" in stages:
            mlp(0)
        if "mix" in stages:
            S.barrier()
            norm_mod(0, 0)
            S.barrier()
            mixer_ab(0)
        if "attn" in stages:
            S.barrier()
            norm_mod(1, 0)
            S.barrier()
            attention(1)

    S.barrier()
    for tt in range(8):
        st, bst = STG[tt % 2], bSTG[tt % 2]
        for kq in range(2):
            ps, bps = next_ps()
            for j in range(4):
                kk = kq * 4 + j
                tr(ps[:, j * 128:(j + 1) * 128], XT[:, kk, tt * 128:(tt + 1) * 128], [bXT], [bps])
            evac(st[:, kq * 512:(kq + 1) * 512], ps[:, :], reads=[bps], writes=[bst])
        S.dma(out=y_out[tt * 128:(tt + 1) * 128, :], in_=st, reads=[bst], queue="act")
    k.close()
    return k


def make_consts():
    c = np.zeros((128, 1024), np.float32)
    c[:, 0:128] = np.eye(128, dtype=np.float32)
    c[:, 128:256] = 1.0 / D
    c[:, 256:384] = 1.0
    c[:, 384] = EPS
    c[0:64, 512:576] = 1.0 / 64
    c[64:128, 576:640] = 1.0 / 64
    c[:, 640:768] = 1.0 / 128
    c[:, 768:896] = np.triu(np.ones((128, 128), np.float32))
    c[:, 896:1024] = np.tril(np.ones((128, 128), np.float32))
    return c


GRID_W, KW_, KH_ = 64, 16, 8


def make_etab(rpb, is_prompt):
    out = np.zeros((8, 128, 2, 14, 64), np.float32)
    if is_prompt:
        return out.reshape(8, 128, 2 * 14 * 64)
    q = np.arange(64)
    kk = np.arange(64)
    cstart = np.clip(q - KW_ // 2, 0, GRID_W - KW_)
    valid = (kk[:, None] >= cstart[None, :]) & (kk[:, None] < cstart[None, :] + KW_)
    cidx = np.clip(kk[:, None] - q[None, :] + KW_ - 1, 0, 2 * KW_ - 2)
    for h in range(16):
        hb, hh = h // 2, h % 2
        for idx in range(14):
            pair = 13 - idx
            for half in range(2):
                dr = pair + half
                tile = np.where(valid, rpb[h, dr][cidx], np.float32(-1e4))
                out[hb, half * 64:(half + 1) * 64, hh, idx, :] = tile
    return out.reshape(8, 128, 2 * 14 * 64)


def make_mk(is_prompt):
    mk = np.zeros((128, 8, 16), np.float32)
    for p in range(128):
        for i in range(8):
            row = 2 * i + (1 if p >= 64 else 0)
            for r in range(16):
                if is_prompt:
                    ok = (row // 4) == (r // 4)
                else:
                    rs = min(max(r - 4, 0), 8)
                    ok = rs <= row < rs + 8
                mk[p, i, r] = 1.0 if ok else 0.0
    return mk.reshape(128, 128)


def core_inputs(ci, inputs):
    if ci < 4:
        x = inputs["x_prompt"][4 * ci:4 * ci + 4].reshape(T, D)
        cond = inputs["c_ctx"]
    else:
        b = ci - 4
        x = inputs["x_sample"][b]
        cond = inputs["c"][b]
    m = {}
    m["x"] = np.ascontiguousarray(x, dtype=np.float32)
    m["condc"] = np.ascontiguousarray(cond.reshape(8, 128).T)
    m["ada_w"] = inputs["ada_w"]
    m["ada_bc"] = np.ascontiguousarray(inputs["ada_b"].reshape(2, 48, 128).transpose(2, 0, 1).reshape(128, 96))
    m["normg"] = np.ascontiguousarray(inputs["norm_g"].reshape(2, 2, 8, 128).transpose(3, 0, 1, 2).reshape(128, 32))
    m["w_mlp1"] = inputs["w_mlp1"]
    m["w_mlp2"] = inputs["w_mlp2"]
    m["consts"] = make_consts()
    c2 = np.zeros((128, 256), np.float32)
    c2[:, 0:128] = -np.triu(np.ones((128, 128), np.float32), 1)
    c2[:, 128:256] = -np.tril(np.ones((128, 128), np.float32), -1)
    m["consts2"] = c2
    c3 = np.zeros((128, 256), np.float32)
    c3[:, 0:64] = 1.0
    c3[:, 128 + 64:256] = 1.0
    m["consts3"] = c3
    is_p = ci < 4
    if "w_in_ab" in inputs:
        m["w_in"] = inputs["w_in_ab"][0]
        m["w_out_ab"] = inputs["w_out_ab"][0]
        m["lb_in"] = np.ascontiguousarray(inputs["hgrn_lb"].reshape(3, 4, 128).transpose(2, 0, 1).reshape(128, 12))
        m["gn_in"] = np.ascontiguousarray(np.stack([inputs["gn_hgrn"][0], inputs["gn_gdn"][0]], axis=1))
        m["keep"] = np.full((128, 1), 0.0 if is_p else 1.0, np.float32)
        sh = np.zeros((4, 2, 4, 128, 128), np.float32)
        sg = np.zeros((4, 2, 4, 128, 128), np.float32)
        if not is_p:
            sh[0, 0] = inputs["state_hgrn"][ci - 4, 0, 0]
            sh[3, 1] = inputs["state_hgrn"][ci - 4, 0, 1]
            sg[0, 0] = inputs["state_gdn"][ci - 4, 0, 0]
            sg[3, 1] = inputs["state_gdn"][ci - 4, 0, 1]
        m["sth_init"] = sh
        m["conv_in"] = np.ascontiguousarray(inputs["gdn_conv"][0, :, 0, :].reshape(5, 12, 128).transpose(2, 1, 0).reshape(128, 60))
        m["albt_in"] = np.ascontiguousarray(np.tile(np.concatenate([inputs["gdn_a_log"][0].reshape(8), inputs["gdn_dt_bias"][0].reshape(8)])[None, :], (128, 1)))
        m["stg_init"] = sg
    if "w_qkv_na" in inputs:
        m["w_qkv"] = inputs["w_qkv_na"][0]
        m["w_out_na"] = inputs["w_out_na"][0]
        m["qkn"] = np.ascontiguousarray(np.stack([np.tile(inputs["qn_na"][0], 2), np.tile(inputs["kn_na"][0], 2)], axis=1))
        m["etab"] = make_etab(inputs["rpb_na"][0], is_p)
        m["mk"] = make_mk(is_p)
        m["mctx"] = np.full((128, 1), 0.0 if is_p else 1.0, np.float32)
        if is_p:
            m["ck_in"] = np.zeros((16, 256, 64), np.float32)
            m["cv_in"] = np.zeros((16, 256, 64), np.float32)
        else:
            m["ck_in"] = np.ascontiguousarray(inputs["cache_na_k"][ci - 4, 0])
            m["cv_in"] = np.ascontiguousarray(inputs["cache_na_v"][ci - 4, 0])
    return m


def kernel(**inputs):
    inputs = {k_: np.asarray(v) for k_, v in inputs.items()}
    kb = build()
    in_maps = []
    for ci in range(NCORES):
        m = core_inputs(ci, inputs)
        in_maps.append({k_: np.ascontiguousarray(v, dtype=np.float32) for k_, v in m.items() if k_ in kb.din})
    res = run_bass_kernel_spmd(kb.nc, in_maps, core_ids=list(range(NCORES)))
    r = res.results
    y_prompt = np.concatenate([r[ci]["y"].reshape(4, 256, D) for ci in range(4)], axis=0)
    y_sample = np.stack([r[ci]["y"] for ci in range(4, 8)], axis=0)
    st_h = np.concatenate([r[ci]["st_h"] for ci in range(4)], axis=0)[:, None]
    st_g = np.concatenate([r[ci]["st_g"] for ci in range(4)], axis=0)[:, None]
    ck = np.concatenate([r[ci]["ck"] for ci in range(4)], axis=0)[:, None]
    cv = np.concatenate([r[ci]["cv"] for ci in range(4)], axis=0)[:, None]
    return (np.ascontiguousarray(y_prompt), np.ascontiguousarray(y_sample), np.ascontiguousarray(st_h),
            np.ascontiguousarray(st_g), np.ascontiguousarray(ck), np.ascontiguousarray(cv))
```
